# Optimizing a Trainium2 kernel written in Bass

```python
import math
import jax, jax.numpy as jnp
from jax import lax
import numpy as np

D_MODEL = 2048
BATCH = 1
SEQ = 8192
DEPTH = 4

DIFF_HEADS = 8
DIFF_QK_DIM = 64
DIFF_V_DIM = 128
DIFF_WIDTH = DIFF_HEADS * DIFF_V_DIM
MLA_HEADS = 8
MLA_Q_RANK = 512
MLA_KV_RANK = 512
MLA_NOPE_DIM = 128
MLA_ROPE_DIM = 64
MLA_V_DIM = 128
MLA_WIDTH = MLA_HEADS * MLA_V_DIM
MIX_WIDTH = DIFF_WIDTH + MLA_WIDTH
IN_SIZES = (
    DIFF_HEADS * 2 * DIFF_QK_DIM,
    DIFF_HEADS * 2 * DIFF_QK_DIM,
    DIFF_WIDTH,
    DIFF_WIDTH,
    MLA_Q_RANK,
    MLA_KV_RANK,
    MLA_ROPE_DIM,
    MLA_WIDTH,
)
IN_WIDTH = sum(IN_SIZES)
ROPE_THETA = 10000.0
NUM_BUCKETS = 32
MAX_DISTANCE = 128
BLOCK_Q = 128
EPS = 1e-6
NEG_INF = -1e30

kernel_name = "hymba_diffattn_mla_hybrid"


def _rms_norm(x, g):
    xf = x.astype(jnp.float32)
    y = xf * lax.rsqrt(jnp.mean(xf * xf, axis=-1, keepdims=True) + EPS)
    return (y * g.astype(jnp.float32)).astype(x.dtype)


def _rope_tables(seq):
    pos = jnp.arange(seq, dtype=jnp.float32)
    inv = 1.0 / (ROPE_THETA ** (jnp.arange(0, MLA_ROPE_DIM, 2, dtype=jnp.float32) / MLA_ROPE_DIM))
    ang = pos[:, None] * inv[None, :]
    ang = jnp.concatenate([ang, ang], axis=-1)
    return jnp.cos(ang), jnp.sin(ang)


def _apply_rope(x, cos, sin):
    half = x.shape[-1] // 2
    x1, x2 = x[..., :half], x[..., half:]
    rot = jnp.concatenate([-x2, x1], axis=-1)
    return x * cos.astype(x.dtype) + rot * sin.astype(x.dtype)


def _t5_bucket(dist):
    n = jnp.maximum(dist, 0)
    max_exact = NUM_BUCKETS // 2
    nf = jnp.maximum(n, 1).astype(jnp.float32)
    large = max_exact + (jnp.log(nf / max_exact) / math.log(MAX_DISTANCE / max_exact)
                         * (NUM_BUCKETS - max_exact)).astype(jnp.int32)
    large = jnp.minimum(large, NUM_BUCKETS - 1)
    return jnp.where(n < max_exact, n, large)


def _block_sweep(one_block, seq):
    starts = jnp.arange(seq // BLOCK_Q, dtype=jnp.int32) * BLOCK_Q
    out = lax.map(one_block, starts)
    out = jnp.moveaxis(out, 0, 1)
    b, nb, q, h, dv = out.shape
    return out.reshape(b, nb * q, h, dv)


def _diff_attention(q, k, v, lam, bias_table):
    seq = q.shape[1]
    kpos = jnp.arange(seq, dtype=jnp.int32)
    scale = DIFF_QK_DIM ** -0.5
    table = bias_table.astype(jnp.float32)

    def one_block(start):
        qb = lax.dynamic_slice_in_dim(q, start, BLOCK_Q, axis=1)
        s = jnp.einsum('bqhmd,bkhmd->bmhqk', qb, k).astype(jnp.float32) * scale
        dist = (start + jnp.arange(BLOCK_Q, dtype=jnp.int32))[:, None] - kpos[None, :]
        bias = jnp.transpose(table[_t5_bucket(dist)], (2, 0, 1))
        s = jnp.where((dist >= 0)[None, None, None], s + bias[None, None], NEG_INF)
        p = jax.nn.softmax(s, axis=-1)
        w = p[:, 0] - lam * p[:, 1]
        return jnp.einsum('bhqk,bkhd->bqhd', w.astype(v.dtype), v)

    return _block_sweep(one_block, seq)


def _mla_attention(q_nope, q_rope, k_nope, k_rope, v):
    seq = q_nope.shape[1]
    kpos = jnp.arange(seq, dtype=jnp.int32)
    scale = (MLA_NOPE_DIM + MLA_ROPE_DIM) ** -0.5

    def one_block(start):
        qn = lax.dynamic_slice_in_dim(q_nope, start, BLOCK_Q, axis=1)
        qr = lax.dynamic_slice_in_dim(q_rope, start, BLOCK_Q, axis=1)
        s = (jnp.einsum('bqhd,bkhd->bhqk', qn, k_nope)
             + jnp.einsum('bqhr,bkr->bhqk', qr, k_rope)).astype(jnp.float32) * scale
        dist = (start + jnp.arange(BLOCK_Q, dtype=jnp.int32))[:, None] - kpos[None, :]
        s = jnp.where((dist >= 0)[None, None], s, NEG_INF)
        p = jax.nn.softmax(s, axis=-1)
        return jnp.einsum('bhqk,bkhd->bqhd', p.astype(v.dtype), v)

    return _block_sweep(one_block, seq)


def setup_inputs(seed: int = 0) -> dict:
    key = jax.random.key(seed)
    ks = jax.random.split(key, 12)
    f32 = jnp.float32
    x = jax.random.normal(ks[0], (BATCH, SEQ, D_MODEL), f32)
    norm_g = 1.0 + 0.02 * jax.random.normal(ks[1], (DEPTH, D_MODEL), f32)
    w_in = jax.random.normal(ks[2], (DEPTH, D_MODEL, IN_WIDTH), f32) * D_MODEL ** -0.5
    diff_lambda = 0.1 * jax.random.normal(ks[3], (DEPTH, 4, DIFF_QK_DIM), f32)
    diff_subln_g = 1.0 + 0.02 * jax.random.normal(ks[4], (DEPTH, DIFF_V_DIM), f32)
    rel_bias_table = 0.2 * jax.random.normal(ks[5], (NUM_BUCKETS, DIFF_HEADS), f32)
    mla_q_norm_g = 1.0 + 0.02 * jax.random.normal(ks[6], (DEPTH, MLA_Q_RANK), f32)
    w_uq = jax.random.normal(ks[7], (DEPTH, MLA_Q_RANK, MLA_HEADS * (MLA_NOPE_DIM + MLA_ROPE_DIM)), f32) * MLA_Q_RANK ** -0.5
    mla_kv_norm_g = 1.0 + 0.02 * jax.random.normal(ks[8], (DEPTH, MLA_KV_RANK), f32)
    w_ukv = jax.random.normal(ks[9], (DEPTH, MLA_KV_RANK, MLA_HEADS * (MLA_NOPE_DIM + MLA_V_DIM)), f32) * MLA_KV_RANK ** -0.5
    w_out = jax.random.normal(ks[10], (DEPTH, MIX_WIDTH, D_MODEL), f32) * MIX_WIDTH ** -0.5
    final_norm_g = 1.0 + 0.02 * jax.random.normal(ks[11], (D_MODEL,), f32)
    return {"x": x, "norm_g": norm_g, "w_in": w_in, "diff_lambda": diff_lambda,
            "diff_subln_g": diff_subln_g, "rel_bias_table": rel_bias_table,
            "mla_q_norm_g": mla_q_norm_g, "w_uq": w_uq, "mla_kv_norm_g": mla_kv_norm_g,
            "w_ukv": w_ukv, "w_out": w_out, "final_norm_g": final_norm_g}


def reference(x, norm_g, w_in, diff_lambda, diff_subln_g, rel_bias_table,
              mla_q_norm_g, w_uq, mla_kv_norm_g, w_ukv, w_out, final_norm_g):
    b, seq, _ = x.shape
    cos, sin = _rope_tables(seq)
    split_points = [int(v) for v in np.cumsum(IN_SIZES)[:-1]]
    for l in range(DEPTH):
        h = _rms_norm(x, norm_g[l])
        proj = jnp.einsum('bsd,de->bse', h, w_in[l])
        q_a, k_a, v_a, z_a, c_q, c_kv, k_r, z_b = jnp.split(proj, split_points, axis=-1)

        lambda_init = 0.8 - 0.6 * math.exp(-0.3 * l)
        lp = diff_lambda[l].astype(jnp.float32)
        lam = (jnp.exp(jnp.sum(lp[0] * lp[1])) - jnp.exp(jnp.sum(lp[2] * lp[3])) + lambda_init)
        qa = q_a.reshape(b, seq, DIFF_HEADS, 2, DIFF_QK_DIM)
        ka = k_a.reshape(b, seq, DIFF_HEADS, 2, DIFF_QK_DIM)
        va = v_a.reshape(b, seq, DIFF_HEADS, DIFF_V_DIM)
        o_a = _diff_attention(qa, ka, va, lam, rel_bias_table)
        o_a = _rms_norm(o_a, diff_subln_g[l]) * (1.0 - lambda_init)
        o_a = o_a.reshape(b, seq, DIFF_WIDTH) * jax.nn.silu(z_a)

        q_b = jnp.einsum('bsr,re->bse', _rms_norm(c_q, mla_q_norm_g[l]), w_uq[l])
        q_b = q_b.reshape(b, seq, MLA_HEADS, MLA_NOPE_DIM + MLA_ROPE_DIM)
        q_nope, q_rope = q_b[..., :MLA_NOPE_DIM], q_b[..., MLA_NOPE_DIM:]
        q_rope = _apply_rope(q_rope, cos[:, None, :], sin[:, None, :])
        kv = jnp.einsum('bsr,re->bse', _rms_norm(c_kv, mla_kv_norm_g[l]), w_ukv[l])
        kv = kv.reshape(b, seq, MLA_HEADS, MLA_NOPE_DIM + MLA_V_DIM)
        k_nope, v_b = kv[..., :MLA_NOPE_DIM], kv[..., MLA_NOPE_DIM:]
        k_rope = _apply_rope(k_r, cos, sin)
        o_b = _mla_attention(q_nope, q_rope, k_nope, k_rope, v_b)
        o_b = o_b.reshape(b, seq, MLA_WIDTH) * jax.nn.silu(z_b)

        mixed = jnp.concatenate([o_a, o_b], axis=-1)
        x = x + jnp.einsum('bse,ed->bsd', mixed, w_out[l])
    return _rms_norm(x, final_norm_g)
```

```python
import math
import numpy as np
import concourse.bass as bass
import concourse.mybir as mybir
from concourse.bass_utils import run_bass_kernel_spmd

F32 = mybir.dt.float32
BF16 = mybir.dt.bfloat16
ALU = mybir.AluOpType
AF = mybir.ActivationFunctionType
AX = mybir.AxisListType

NCORES = 8
D = 2048
SEQ = 8192
T = SEQ // NCORES
NJ = T // 128
DEPTH = 4
KC = D // 128
EPS = 1e-6
NEG = -30000.0
NBLK = 18
KA_OFF, VA_OFF = 0, 8192
KN_OFF, KR_OFF, VB_OFF = 0, 8192, 9216
COLS_A, COLS_B = 16384, 17408
SC_D = 64 ** -0.5
SC_M = 192 ** -0.5
B_KA, B_VA, B_CKV, B_WUKV, B_QA, B_ZA, B_CQ, B_WUQ, B_ZB, B_WO = 0, 2, 4, 5, 6, 8, 10, 11, 12, 14


class Eng:
    def __init__(self, name):
        self.name = name
        self.ops = []
        self.meta = []
        self.waited = {}


class _Stop(Exception):
    pass


class Sem:
    def __init__(self, h):
        self.h = h
        self.n = 0


def build(depth=DEPTH, debug=False, stop=None):
    nc = bass.Bass("TRN2", target_bir_lowering=False)
    dt_in = lambda name, shape, dt=F32: nc.dram_tensor(name, list(shape), dt, kind="ExternalInput").ap()
    xT_d = dt_in("xT", [D, T])
    wpack_d = dt_in("wpack", [depth, NBLK, 128, 8192])
    wkr_d = dt_in("wkr", [depth, 128, 2048])
    gx_d = dt_in("gx", [128, DEPTH * KC])
    gq_d = dt_in("gq", [128, DEPTH * 4])
    gkv_d = dt_in("gkv", [128, DEPTH * 4])
    gfin_d = dt_in("gfin", [128, KC])
    subln_d = dt_in("subln", [128, DEPTH * 128])
    lam_d = dt_in("lam", [128, DEPTH * 256])
    b31_d = dt_in("b31", [128, 9])
    cos_d = dt_in("cosT", [64, T])
    sin_d = dt_in("sinT", [64, T])
    bm_d = dt_in("bm", [128, 9, 9 * 128])
    ident_d = dt_in("ident", [128, 128])
    yT_d = nc.dram_tensor("yT", [D, T], F32, kind="ExternalOutput").ap()

    kvsrcA = [nc.dram_tensor(f"kvsrcA{i}", [128, COLS_A], BF16) for i in range(2)]
    kvdstA = [nc.dram_tensor(f"kvdstA{i}", [NCORES * 128, COLS_A], BF16) for i in range(2)]
    kvsrcB = [nc.dram_tensor(f"kvsrcB{i}", [128, COLS_B], BF16) for i in range(2)]
    kvdstB = [nc.dram_tensor(f"kvdstB{i}", [NCORES * 128, COLS_B], BF16) for i in range(2)]
    qbuf = nc.dram_tensor("qbuf", [128, 24, T], BF16).ap()
    zbuf = nc.dram_tensor("zbuf", [128, NJ, 2048], F32).ap()
    bmbuf = nc.dram_tensor("bmbuf", [128, 9, 9 * 128], BF16).ap()
    if debug:
        dbg_q = nc.dram_tensor("dbg_q", [128, 24, T], BF16, kind="ExternalOutput").ap()
        dbg_z = nc.dram_tensor("dbg_z", [128, NJ, 2048], F32, kind="ExternalOutput").ap()
        dbg_m = nc.dram_tensor("dbg_m", [128, KC, T], BF16, kind="ExternalOutput").ap()
        dbg_x = nc.dram_tensor("dbg_x", [128, KC, T], F32, kind="ExternalOutput").ap()

    PE, ACT, DVE, POOL, SP = Eng("pe"), Eng("act"), Eng("dve"), Eng("pool"), Eng("sp")

    def emit(E, fn, inc=None, k=1):
        if inc is None:
            E.ops.append(fn)
            E.meta.append(("inst", None, 0))
            return None
        inc.n += k
        E.ops.append(lambda e: fn(e).then_inc(inc.h, k))
        E.meta.append(("inst", inc, k))
        return inc.n

    def wait(E, sem, val):
        if sem is None or val is None or val <= 0:
            return
        if E.waited.get(id(sem), 0) >= val:
            return
        E.waited[id(sem)] = val
        E.ops.append(lambda e: e.wait_ge(sem.h, val))
        E.meta.append(("wait", sem, val))

    import contextlib
    with contextlib.ExitStack() as es:
        def sb(name, shape, dt):
            return es.enter_context(nc.sbuf_tensor(name, list(shape), dt))

        def ps(name, shape, dt):
            return es.enter_context(nc.psum_tensor(name, list(shape), dt))

        sems = {}

        def S(name):
            s = Sem(es.enter_context(nc.semaphore(name)))
            sems[name] = s
            return s

        xT = sb("xT_sb", [128, KC, T], F32)
        hT = sb("hT_sb", [128, KC, T], BF16)
        mixedT = hT
        Vt = [sb(f"Vt{i}", [128, 8, 130], BF16) for i in range(3)]
        gx = sb("gx_sb", [128, DEPTH * KC], F32)
        gq = sb("gq_sb", [128, DEPTH * 4], F32)
        gkv = sb("gkv_sb", [128, DEPTH * 4], F32)
        gfin = sb("gfin_sb", [128, KC], F32)
        subln = sb("subln_sb", [128, DEPTH * 128], F32)
        neglam = sb("neglam_sb", [128, DEPTH], F32)
        b31 = sb("b31_sb", [128, 9], F32)
        negb31 = sb("negb31_sb", [128, 9], F32)
        cosT = sb("cos_sb", [64, T], F32)
        sinT = sb("sin_sb", [64, T], F32)
        ident = sb("ident_sb", [128, 128], BF16)
        ones = sb("ones_sb", [128, 128], BF16)
        epsT = sb("eps_sb", [128, 1], F32)
        small = sb("small_sb", [128, 64], F32)
        mixt = [sb(f"mixt{i}", [128, 8, 128], BF16) for i in range(2)]
        BMt = [sb(f"BMt{i}", [128, 9, 128], BF16) for i in range(2)]
        POOLA_B = 36864
        poolA = sb("poolA", [128, POOLA_B // 2], BF16)
        POOLF_B = 34816
        poolF = sb("poolF", [128, POOLF_B // 2], BF16)

        def view(pool, off_b, shape, dt):
            n = int(np.prod(shape[1:]))
            nb = n * (4 if dt == F32 else 2)
            a = pool[0:shape[0], off_b // 2:(off_b + nb) // 2]
            if dt == F32:
                a = a.bitcast(F32)
            if len(shape) == 3:
                a = a.rearrange("p (a b) -> p a b", a=shape[1])
            return a

        wslot = [view(poolA, i * 16384, [128, 8192], BF16) for i in range(2)]
        wkr_t = view(poolA, 32768, [128, 2048], BF16)
        Kt = [view(poolA, i * 2048, [128, 8, 128], BF16) for i in range(3)]
        KRt = [view(poolA, 6144 + i * 2048, [64, 8, 128], BF16) for i in range(3)]
        Qt = [view(poolA, 12288 + i * 2048, [128, T], BF16) for i in range(2)]
        QRt = [view(poolA, 16384 + i * 2048, [64, T], BF16) for i in range(2)]
        NP = 4
        Pt = [view(poolA, 20480 + i * 2048, [128, 1024], BF16) for i in range(NP)]
        cg = view(poolF, 0, [128, 4, T], F32)
        cn = view(poolF, 16384, [128, 4, T], BF16)
        sqc = [view(poolF, 24576 + i * 1024, [128, 512], BF16) for i in range(4)]
        rope_t = [view(poolF, 24576 + i * 2048, [64, 512], F32) for i in range(2)]
        rstd = view(poolF, 28672, [128, T], F32)
        tmpf = view(poolF, 32768, [128, 512], F32)
        zt = [view(poolF, i * 4096, [128, 8, 128], F32) for i in range(2)]
        o1buf = view(poolF, 8192, [128, 8, 128], F32)
        obuf = view(poolF, 12288, [128, 8, 128], F32)
        sqt = view(poolF, 16384, [128, 128], F32)
        zg = view(poolF, 16896, [128, 128], F32)
        accs = view(poolF, 17408, [128, 8, 129], F32)
        stb = [sb(f"stb{i}", [128, 1024], BF16) for i in range(4)]
        stf = [sb(f"stf{i}", [128, 512], F32) for i in range(2)]

        ps_s = [ps(f"ps_s{i}", [128, 1024], F32) for i in range(2)]
        ps_acc = ps("ps_acc", [128, 1536], F32)
        ps_tb = ps("ps_tb", [128, 1024], BF16)

        def bank(i):
            i = i % 7
            if i < 4:
                return ps_s[i // 2][:, (i % 2) * 512:(i % 2) * 512 + 512]
            return ps_acc[:, (i - 4) * 512:(i - 4) * 512 + 512]

        def acc(j):
            o = (j // 3) * 512 + (j % 3) * 129
            return ps_acc[:, o:o + 129]

        s_pre = S("pre")
        s_prep = S("prep")
        s_mm = S("mm")
        s_dve = S("dve")
        s_act = S("act")
        s_pool = S("pool")
        s_w = [S(f"w{i}") for i in range(2)]
        s_wkr = S("wkr")
        s_stb = [S(f"stb{i}") for i in range(4)]
        s_stf = [S(f"stf{i}") for i in range(2)]
        s_ccA = [S(f"ccA{i}") for i in range(depth)]
        s_ccB = [S(f"ccB{i}") for i in range(depth)]
        s_kv = [S(f"kv{i}") for i in range(3)]
        s_q = [S(f"q{i}") for i in range(2)]
        s_z = [S(f"z{i}") for i in range(2)]
        s_bm = [S(f"bmm{i}") for i in range(2)]
        s_qk = S("qk")
        s_exp = S("exp")
        s_pv = S("pv")
        s_tr = S("tr")
        s_out = S("out")
        s_misc = S("misc")

        def MM(out, lhsT, rhs, start=True, stop=True, skip=False, inc=None):
            return emit(PE, lambda e: e.matmul(out, lhsT, rhs, start=start, stop=stop, skip_group_check=skip), inc)

        def ACTV(out, in_, func, bias=None, scale=1.0, inc=None):
            if bias is None:
                return emit(ACT, lambda e: e.activation(out=out, in_=in_, func=func, scale=scale), inc)
            return emit(ACT, lambda e: e.activation(out=out, in_=in_, func=func, bias=bias, scale=scale), inc)

        def TT(E, out, in0, in1, op, inc=None):
            return emit(E, lambda e: e.tensor_tensor(out=out, in0=in0, in1=in1, op=op), inc)

        def TS(E, out, in0, s1, op0, s2=None, op1=None, inc=None):
            if op1 is None:
                return emit(E, lambda e: e.tensor_scalar(out=out, in0=in0, scalar1=s1, scalar2=None, op0=op0), inc)
            return emit(E, lambda e: e.tensor_scalar(out=out, in0=in0, scalar1=s1, scalar2=s2, op0=op0, op1=op1), inc)

        def STT(E, out, in0, scalar, in1, op0, op1, inc=None):
            return emit(E, lambda e: e.scalar_tensor_tensor(out=out, in0=in0, scalar=scalar, in1=in1, op0=op0, op1=op1), inc)

        def CP(E, out, in_, inc=None):
            return emit(E, lambda e: e.tensor_copy(out=out, in_=in_), inc)

        def DMA(E, out, in_, inc, k=16):
            return emit(E, lambda e: e.dma_start(out=out, in_=in_), inc, k)

        xT_v = xT_d.rearrange("(kc p) t -> p kc t", p=128)
        for q in range(4):
            DMA(SP, xT[:, 4 * q:4 * q + 4, :], xT_v[:, 4 * q:4 * q + 4, :], s_pre)
        for dst, src in ((gx, gx_d), (gq, gq_d), (gkv, gkv_d), (gfin, gfin_d), (subln, subln_d),
                         (b31, b31_d), (cosT, cos_d), (sinT, sin_d)):
            DMA(SP, dst[:], src, s_pre)
        lam_t = view(poolF, 0, [128, DEPTH * 256], F32)
        DMA(SP, lam_t, lam_d, s_pre)
        pre_all = s_pre.n
        DMA(POOL, ident[:], ident_d, s_prep)
        emit(POOL, lambda e: e.memset(ones[:], 1.0))
        emit(POOL, lambda e: e.memset(epsT[:], EPS))
        for i in range(3):
            emit(POOL, lambda e, i=i: e.memset(Vt[i][:, :, 128:130], 1.0))
        v_pool_init = emit(POOL, lambda e: e.memset(small[:], 0.0), s_pool)
        wait(POOL, s_prep, s_prep.n)
        wait(PE, s_prep, s_prep.n)
        wait(PE, s_pool, v_pool_init)

        wait(DVE, s_pre, pre_all)
        wait(DVE, s_pool, v_pool_init)
        wait(ACT, s_pre, pre_all)
        wait(ACT, s_pool, v_pool_init)
        lam_tmp = view(poolF, 8192, [128, 64], F32)
        for l in range(depth):
            li = 0.8 - 0.6 * math.exp(-0.3 * l)
            for t in range(2):
                v = TT(DVE, lam_tmp, lam_t[:, l * 256 + t * 128:l * 256 + t * 128 + 64],
                       lam_t[:, l * 256 + t * 128 + 64:l * 256 + t * 128 + 128], ALU.mult, inc=s_dve)
                wait(DVE, s_dve, v)
                v = emit(DVE, lambda e, c=l * 2 + t: e.reduce_sum(out=small[:, 32 + c:33 + c], in_=lam_tmp, axis=AX.X), s_dve)
                wait(DVE, s_dve, v)
            wait(ACT, s_dve, v)
            v = ACTV(small[:, 40 + 2 * l:42 + 2 * l], small[:, 32 + 2 * l:34 + 2 * l], AF.Exp, inc=s_act)
            wait(DVE, s_act, v)
            v = TT(DVE, small[:, 48 + l:49 + l], small[:, 40 + 2 * l:41 + 2 * l], small[:, 41 + 2 * l:42 + 2 * l], ALU.subtract, inc=s_dve)
            wait(DVE, s_dve, v)
            TS(DVE, neglam[:, l:l + 1], small[:, 48 + l:49 + l], li, ALU.add, -1.0, ALU.mult)
            TS(DVE, subln[:, l * 128:(l + 1) * 128], subln[:, l * 128:(l + 1) * 128], 1.0 - li, ALU.mult)
        v = TS(DVE, negb31[:], b31[:], -1.0, ALU.mult, inc=s_dve)
        wait(DVE, s_dve, v)
        bm_f = view(poolF, 16384, [128, 9 * 128], F32)
        bm_b = view(poolF, 16384 + 4608, [128, 9 * 128], BF16)
        v_bmst = 0
        for h in range(9):
            v_ld = DMA(SP, bm_f, bm_d[:, h, :], s_misc)
            wait(DVE, s_misc, v_ld)
            wait(DVE, s_out, v_bmst)
            v = TS(DVE, bm_b, bm_f, negb31[:, h:h + 1], ALU.add, (1.0 / SC_D) if h < 8 else (1.0 / SC_M), ALU.mult, inc=s_dve)
            wait(SP, s_dve, v)
            v_bmst = DMA(SP, bmbuf[:, h, :], bm_b, s_out)
            wait(SP, s_out, v_bmst)
        pre_done_dve = s_dve.n

        gstate = {"g": 0, "evac": {}}

        def group_begin():
            g = gstate["g"]
            for (sem, val) in gstate["evac"].pop(g - 7, []):
                wait(PE, sem, val)
            gstate["g"] = g + 1
            return g

        def group_end(g, evacs):
            gstate["evac"][g] = evacs

        def drain_groups():
            for g in sorted(gstate["evac"].keys()):
                for (sem, val) in gstate["evac"][g]:
                    wait(PE, sem, val)
            gstate["evac"].clear()

        stb_state = {"i": 0, "last": [0] * 4}
        stf_state = {"i": 0, "last": [0] * 2}

        def stb_acquire(E):
            i = stb_state["i"] % 4
            stb_state["i"] += 1
            wait(E, s_stb[i], stb_state["last"][i])
            return i

        def stb_store(i, dst, src, after):
            wait(SP, after[0], after[1])
            stb_state["last"][i] = DMA(SP, dst, src, s_stb[i])

        def stf_acquire(E):
            i = stf_state["i"] % 2
            stf_state["i"] += 1
            wait(E, s_stf[i], stf_state["last"][i])
            return i

        def stf_store(i, dst, src, after):
            wait(SP, after[0], after[1])
            stf_state["last"][i] = DMA(SP, dst, src, s_stf[i])

        wstate = {"n": 0, "free": {}, "loaded": {}}

        def wload(l, b):
            n = wstate["n"]
            wstate["n"] += 1
            slot = n % 2
            fr = wstate["free"].get(n - 2)
            if fr is not None:
                wait(POOL, fr[0], fr[1])
            v = DMA(POOL, wslot[slot], wpack_d[l, b], s_w[slot])
            wstate["loaded"][(l, b)] = (slot, v)
            return n

        wseq = {}

        def wuse(l, b):
            slot, v = wstate["loaded"][(l, b)]
            wait(PE, s_w[slot], v)
            return wslot[slot]

        def wfree(l, b, semval):
            wstate["free"][wseq[(l, b)]] = semval

        def fm_group(W3, kcn, c0, M, src, hf, out_rows=None):
            g = group_begin()
            o = bank(g)[0:M, :]
            v = None
            for kc in range(kcn):
                v = MM(o, W3[:, kc, c0:c0 + M], src[:, kc, hf * 512:(hf + 1) * 512],
                       start=(kc == 0), stop=(kc == kcn - 1), inc=(s_mm if kc == kcn - 1 else None))
            return g, o, v

        def tm_group(rhs_fn, kcn, src, j):
            g = group_begin()
            o = bank(g)
            v = None
            for kc in range(kcn):
                v = MM(o, src[:, kc, j * 128:(j + 1) * 128], rhs_fn(kc),
                       start=(kc == 0), stop=(kc == kcn - 1), inc=(s_mm if kc == kcn - 1 else None))
            return g, o, v

        rstd_last_read = {"v": (None, None)}

        def stats_group(sq_list, n, hf, ):
            g = group_begin()
            o = bank(g)
            v = None
            for i, (sq_ap, after) in enumerate(sq_list):
                wait(PE, after[0], after[1])
                v = MM(o, ones[:], sq_ap, start=(i == 0), stop=(i == len(sq_list) - 1),
                       inc=(s_mm if i == len(sq_list) - 1 else None))
            wait(ACT, s_mm, v)
            wait(ACT, rstd_last_read["v"][0], rstd_last_read["v"][1])
            v1 = ACTV(tmpf, o, AF.Ln, bias=epsT[:, 0:1], scale=1.0 / n, inc=s_act)
            wait(ACT, s_act, v1)
            v2 = ACTV(rstd[:, hf * 512:(hf + 1) * 512], tmpf, AF.Exp, scale=-0.5, inc=s_act)
            group_end(g, [(s_act, v1)])
            return v2

        import os
        KVAR = 3

        def rope_evac(gA, oA, vA, gB, oB, vB, hf, dst):
            wait(DVE, s_mm, vB)
            if KVAR == 1:
                v = CP(DVE, dst, oA, inc=s_dve)
                group_end(gA, [(s_dve, v)])
                group_end(gB, [(s_dve, v)])
                return v
            if KVAR == 3:
                CP(DVE, rope_t[0], oA)
                v = CP(DVE, rope_t[1], oB, inc=s_dve)
                group_end(gA, [(s_dve, v)])
                group_end(gB, [(s_dve, v)])
                wait(DVE, s_dve, v)
                TT(DVE, rope_t[0], rope_t[0], cosT[:, hf * 512:(hf + 1) * 512], ALU.mult)
                v = TT(DVE, rope_t[1], rope_t[1], sinT[:, hf * 512:(hf + 1) * 512], ALU.mult, inc=s_dve)
                wait(DVE, s_dve, v)
                return TT(DVE, dst, rope_t[0], rope_t[1], ALU.add, inc=s_dve)
            if KVAR == 4:
                TT(DVE, rope_t[0], oA, rstd[0:64, hf * 512:(hf + 1) * 512], ALU.mult)
                v = TT(DVE, rope_t[1], oB, rstd[0:64, hf * 512:(hf + 1) * 512], ALU.mult, inc=s_dve)
                group_end(gA, [(s_dve, v)])
                group_end(gB, [(s_dve, v)])
                wait(DVE, s_dve, v)
                return TT(DVE, dst, rope_t[0], rope_t[1], ALU.add, inc=s_dve)
            if KVAR == 2:
                CP(DVE, rope_t[0], oA)
                v = CP(DVE, rope_t[1], oB, inc=s_dve)
                group_end(gA, [(s_dve, v)])
                group_end(gB, [(s_dve, v)])
                wait(DVE, s_dve, v)
                return TT(DVE, dst, rope_t[0], rope_t[1], ALU.add, inc=s_dve)
            TT(DVE, rope_t[0], oA, cosT[:, hf * 512:(hf + 1) * 512], ALU.mult)
            v = TT(DVE, rope_t[1], oB, sinT[:, hf * 512:(hf + 1) * 512], ALU.mult, inc=s_dve)
            group_end(gA, [(s_dve, v)])
            group_end(gB, [(s_dve, v)])
            wait(DVE, s_dve, v)
            return TT(DVE, dst, rope_t[0], rope_t[1], ALU.add, inc=s_dve)

        x_ready = {"v": (s_pre, pre_all)}

        def x_stats_and_hT(l, g_tile, g_off, out_fn):
            sq_free = [(None, None)] * 4
            for hf in range(2):
                g = group_begin()
                o = bank(g)
                vm = None
                for kc in range(KC):
                    b = kc % 4
                    wait(ACT, x_ready["v"][0], x_ready["v"][1])
                    wait(ACT, sq_free[b][0], sq_free[b][1])
                    v = ACTV(sqc[b], xT[:, kc, hf * 512:(hf + 1) * 512], AF.Square, inc=s_act)
                    wait(PE, s_act, v)
                    vm = MM(o, ones[:], sqc[b], start=(kc == 0), stop=(kc == KC - 1), inc=s_mm)
                    sq_free[b] = (s_mm, vm)
                wait(ACT, s_mm, vm)
                wait(ACT, rstd_last_read["v"][0], rstd_last_read["v"][1])
                v1 = ACTV(tmpf, o, AF.Ln, bias=epsT[:, 0:1], scale=1.0 / D, inc=s_act)
                wait(ACT, s_act, v1)
                v2 = ACTV(rstd[:, hf * 512:(hf + 1) * 512], tmpf, AF.Exp, scale=-0.5, inc=s_act)
                group_end(g, [(s_act, v1)])
                wait(DVE, s_act, v2)
                wait(DVE, x_ready["v"][0], x_ready["v"][1])
                for kc in range(KC):
                    out_fn(kc, hf)


        sqc_free = {"v": (None, None)}

        def c_stage(l, W3, gvec, after_w):
            vlast = None
            for hf in range(2):
                sql = []
                for rc in range(4):
                    g, o, v = fm_group(W3, KC, rc * 128, 128, hT, hf)
                    wait(DVE, s_mm, v)
                    vd = TS(DVE, cg[:, rc, hf * 512:(hf + 1) * 512], o, gvec[:, l * 4 + rc:l * 4 + rc + 1], ALU.mult, inc=s_dve)
                    wait(ACT, s_dve, vd)
                    wait(ACT, sqc_free["v"][0], sqc_free["v"][1])
                    va = ACTV(sqc[rc], o, AF.Square, inc=s_act)
                    group_end(g, [(s_dve, vd), (s_act, va)])
                    sql.append((sqc[rc], (s_act, va)))
                if hf == 1:
                    after_w()
                v2 = stats_group(sql, 512, hf)
                sqc_free["v"] = (s_mm, s_mm.n)
                wait(DVE, s_act, v2)
                for rc in range(4):
                    vlast = TT(DVE, cn[:, rc, hf * 512:(hf + 1) * 512], cg[:, rc, hf * 512:(hf + 1) * 512],
                               rstd[:, hf * 512:(hf + 1) * 512], ALU.mult, inc=s_dve)
            rstd_last_read["v"] = (s_dve, vlast)
            return vlast

        ast = {"ti": 0, "rg": 0, "hl": 0, "ou": 0, "ug": 0,
               "val_qk": {}, "val_exp": {}, "val_pv": {}, "kv_last_tile": {}, "q_last": {}, "zfree": {}, "trf": 0}
        s_accfree = S("accfree")
        s_e1 = S("e1")
        s_e2 = S("e2")
        s_e3 = S("e3")
        s_trf = S("trf")

        NPI = 8
        Pi = [view(poolA, 20480 + i * 1024, [128, 512], BF16) for i in range(NPI)]
        Sbank = [ps_s[0][:, 0:512], ps_s[0][:, 512:1024], ps_s[1][:, 0:512], ps_s[1][:, 512:1024]]
        LA = 2

        def attention_layer(l, ccA_wait, ccB_wait, alias_wait):
            kdA = kvdstA[l % 2].ap()
            kdB = kvdstB[l % 2].ap()
            units = [("d", h, m) for h in range(8) for m in range(2)] + [("m", h, 0) for h in range(8)]
            items = []
            for u in range(len(units)):
                for J in range(8):
                    N = (8 - J) * 128
                    for m in range(8):
                        if N > 512:
                            items.append((u, J, m, 0, 0, 512, False))
                            items.append((u, J, m, 1, 512, N, True))
                        else:
                            items.append((u, J, m, 0, 0, N, True))
            ti0 = ast["ti"]
            n_t = len(items)
            info = {}
            kvinfo = {}
            dve_q = []
            act_q = []
            cur = {"idx": 0}
            rg_list = [(u, J) for u in range(len(units)) for J in range(8)]
            rg_next = {"i": 0}

            wait(SP, ccA_wait[0], ccA_wait[1])
            for (sem, val) in alias_wait:
                wait(SP, sem, val)

            def sp_head_loads(u):
                kind, h, mp = units[u]
                hl = ast["hl"]
                ast["hl"] += 1
                par = hl % 2
                lastq = ast["q_last"].get(hl - 2)
                if lastq is not None:
                    wait(SP, s_qk, lastq)
                if kind == "d":
                    DMA(SP, Qt[par], qbuf[:, h, :], s_q[par])
                else:
                    DMA(SP, Qt[par], qbuf[:, 8 + h, :], s_q[par])
                    DMA(SP, QRt[par], qbuf[0:64, 16 + h, :], s_q[par])
                hidx = h if kind == "d" else 8
                DMA(SP, BMt[par][:], bmbuf[:, hidx, :].rearrange("p (s q) -> p s q", s=9), s_q[par])
                d = {"par": par, "qv": s_q[par].n, "hl": hl}
                ou = ast["ou"]
                ast["ou"] += 1
                zp = ou % 2
                dprev = ast.setdefault("zinfo", {}).get(ou - 2)
                if dprev is not None:
                    assert len(dprev["e3"]) == 8, len(dprev["e3"])
                    wait(SP, dprev["e3"][-1][0], dprev["e3"][-1][1])
                col = h * 128 if kind == "d" else 1024 + h * 128
                DMA(SP, zt[zp], zbuf[:, :, col:col + 128], s_z[zp])
                d.update({"zpar": zp, "zv": s_z[zp].n, "ou": ou, "e3": []})
                ast["zinfo"][ou] = d
                info[u] = d
                if kind == "d":
                    info[u + 1] = d

            def sp_kv_load():
                i = rg_next["i"]
                if i >= len(rg_list):
                    return
                rg_next["i"] += 1
                u, J = rg_list[i]
                kind, h, mp = units[u]
                r = ast["rg"]
                ast["rg"] += 1
                slot = r % 3
                lt = ast["kv_last_tile"].get(r - 3)
                if lt is not None:
                    wait(SP, s_pv, ast["val_pv"][lt])
                if kind == "d":
                    kd = kdA
                    c0 = KA_OFF + h * 1024 + J * 128
                    v0 = VA_OFF + J * 1024 + h * 128
                else:
                    wait(SP, ccB_wait[0], ccB_wait[1])
                    kd = kdB
                    c0 = KN_OFF + h * 1024 + J * 128
                    v0 = VB_OFF + J * 1024 + h * 128
                    c1 = KR_OFF + J * 128
                    DMA(SP, KRt[slot], kd[:, c1:c1 + 128].rearrange("(m p) c -> p m c", p=128)[0:64], s_kv[slot])
                DMA(SP, Kt[slot], kd[:, c0:c0 + 128].rearrange("(m p) c -> p m c", p=128), s_kv[slot])
                DMA(SP, Vt[slot][:, :, 0:128], kd[:, v0:v0 + 128].rearrange("(m p) c -> p m c", p=128), s_kv[slot])
                kvinfo[(u, J)] = (slot, s_kv[slot].n, r)

            def emit_qk(k):
                u, J, m, ci, c0, c1, lastc = items[k]
                kind, h, mp = units[u]
                gk = ti0 + k
                if u not in info:
                    sp_head_loads(u)
                if J == 0 and m == 0 and ci == 0:
                    wait(PE, s_q[info[u]["par"]], info[u]["qv"])
                slot, kvv, r = kvinfo[(u, J)]
                if m == 0 and ci == 0:
                    wait(PE, s_kv[slot], kvv)
                par = info[u]["par"]
                Sb = Sbank[gk % 4]
                q0 = J * 128
                w = c1 - c0
                ops = []
                if kind == "d":
                    ops.append((Sb[:, 0:w], Kt[slot][64 * mp:64 * mp + 64, m, :],
                                Qt[par][64 * mp:64 * mp + 64, q0 + c0:q0 + c1], True))
                else:
                    ops.append((Sb[:, 0:w], Kt[slot][:, m, :], Qt[par][:, q0 + c0:q0 + c1], True))
                    ops.append((Sb[:, 0:w], KRt[slot][:, m, :], QRt[par][:, q0 + c0:q0 + c1], False))
                if ci == 0:
                    ops.append((Sb[:, 0:128], ident[:], BMt[par][:, m, :], False))
                    if m == 7 and J < 7:
                        ops.append((Sb[:, 128:256], ident[:], BMt[par][:, 8, :], False))
                v = None
                for i, (o, lt, rh, st) in enumerate(ops):
                    v = MM(o, lt, rh, start=st, stop=True, skip=True, inc=(s_qk if i == len(ops) - 1 else None))
                ast["val_qk"][gk] = v
                ast["q_last"][info[u]["hl"]] = v

            def emit_exp(k):
                u, J, m, ci, c0, c1, lastc = items[k]
                kind, h, mp = units[u]
                gk = ti0 + k
                wait(ACT, s_qk, ast["val_qk"][gk])
                if gk - NPI in ast["val_pv"]:
                    wait(ACT, s_pv, ast["val_pv"][gk - NPI])
                w = c1 - c0
                hidx = h if kind == "d" else 8
                sc = SC_D if kind == "d" else SC_M
                ast["val_exp"][gk] = ACTV(Pi[gk % NPI][:, 0:w], Sbank[gk % 4][:, 0:w], AF.Exp,
                                          bias=b31[:, hidx:hidx + 1], scale=sc, inc=s_exp)

            def emit_pv(k):
                u, J, m, ci, c0, c1, lastc = items[k]
                kind, h, mp = units[u]
                gk = ti0 + k
                wait(PE, s_exp, ast["val_exp"][gk])
                wait(PE, s_accfree, ast.get("acc_wait", 0))
                slot, kvv, r = kvinfo[(u, J)]
                P = Pi[gk % NPI]
                j0 = J + c0 // 128
                j1 = J + c1 // 128
                v = None
                for j in range(j0, j1):
                    off = (j - J) * 128 - c0
                    v = MM(acc(j), P[:, off:off + 128], Vt[slot][:, m, 0:129],
                           start=(J == 0 and m == 0 and j % 3 == 0), stop=(j == J and m == 7), skip=True,
                           inc=(s_pv if j == j1 - 1 else None))
                ast["val_pv"][gk] = v
                if m == 7 and ci == 0:
                    wait(DVE, s_pv, v)
                    ast["acc_wait"] = CP(DVE, accs[:, J, :], acc(J), inc=s_accfree)
                    schedule_epilogue(k, u, J, ast["acc_wait"])
                if m == 7 and lastc:
                    ast["kv_last_tile"][r] = gk
                    sp_kv_load()
            def schedule_epilogue(idx, u, j, v_pv):
                kind, h, mp = units[u]
                a = accs[:, j, :]
                base = (j % 2) * 8
                rec = small[:, base:base + 1]
                rec2 = small[:, base + 1:base + 2]
                ss = small[:, base + 2:base + 3]
                lnv = small[:, base + 3:base + 4]
                rs = small[:, base + 4:base + 5]
                d = info[u]
                zp = d["zpar"]
                mt = mixt[d["ou"] % 2]
                if kind == "d" and mp == 0:
                    def f1():
                        wait(DVE, s_accfree, v_pv)
                        v = emit(DVE, lambda e: e.reciprocal(out=rec, in_=a[:, 128:129]), s_dve)
                        wait(DVE, s_dve, v)
                        TS(DVE, o1buf[:, j, :], a[:, 0:128], rec, ALU.mult)
                    dve_q.append((idx + 1, f1))
                elif kind == "d":
                    def f1():
                        wait(DVE, s_accfree, v_pv)
                        v = emit(DVE, lambda e: e.reciprocal(out=rec, in_=a[:, 128:129]), s_dve)
                        wait(DVE, s_dve, v)
                        v = TS(DVE, rec2, rec, neglam[:, l:l + 1], ALU.mult, inc=s_dve)
                        wait(DVE, s_dve, v)
                        v = STT(DVE, obuf[:, j, :], a[:, 0:128], rec2, o1buf[:, j, :], ALU.mult, ALU.add, inc=s_dve)
                        wait(DVE, s_dve, v)
                        v = TT(DVE, sqt, obuf[:, j, :], obuf[:, j, :], ALU.mult, inc=s_dve)
                        wait(DVE, s_dve, v)
                        v1 = emit(DVE, lambda e: e.reduce_sum(out=ss, in_=sqt, axis=AX.X), s_e1)

                        def f2():
                            wait(ACT, s_e1, v1)
                            va = ACTV(lnv, ss, AF.Ln, bias=epsT[:, 0:1], scale=1.0 / 128, inc=s_act)
                            wait(ACT, s_act, va)
                            v2 = ACTV(rs, lnv, AF.Exp, scale=-0.5, inc=s_e2)

                            def f3():
                                wait(DVE, s_z[zp], d["zv"])
                                v = TT(DVE, zg, zt[zp][:, j, :], subln[:, l * 128:(l + 1) * 128], ALU.mult, inc=s_dve)
                                wait(DVE, s_dve, v)
                                wait(DVE, s_e2, v2)
                                ve = STT(DVE, mt[:, j, :], obuf[:, j, :], rs, zg, ALU.mult, ALU.mult, inc=s_e3)
                                d["e3"].append((s_e3, ve))
                            dve_q.append((cur["idx"] + 2, f3))
                        act_q.append((cur["idx"] + 2, f2))
                    dve_q.append((idx + 1, f1))
                else:
                    def f1():
                        wait(DVE, s_accfree, v_pv)
                        wait(DVE, s_z[zp], d["zv"])
                        v = emit(DVE, lambda e: e.reciprocal(out=rec, in_=a[:, 128:129]), s_dve)
                        wait(DVE, s_dve, v)
                        ve = STT(DVE, mt[:, j, :], a[:, 0:128], rec, zt[zp][:, j, :], ALU.mult, ALU.mult, inc=s_e3)
                        d["e3"].append((s_e3, ve))
                    dve_q.append((idx + 1, f1))

            def flush(q, idx, force=False):
                while q and (force or q[0][0] <= idx):
                    _, fn = q.pop(0)
                    fn()

            def emit_transposes(u):
                kind, h, mp = units[u]
                d = info[u]
                mt = mixt[d["ou"] % 2]
                assert len(d["e3"]) == 8, (u, len(d["e3"]))
                sem, val = d["e3"][-1]
                wait(PE, sem, val)
                wait(PE, s_trf, ast["trf"])
                v = None
                for j in range(8):
                    v = emit(PE, lambda e, j=j: e.transpose(ps_tb[:, j * 128:(j + 1) * 128], mt[:, j, :], ident[:]),
                             s_tr if j == 7 else None)
                chunk = h if kind == "d" else 8 + h
                wait(DVE, s_tr, v)
                vv = CP(DVE, mixedT[:, chunk, :], ps_tb[:, :], inc=s_trf)
                ast["trf"] = vv
                ast["zfree"][d["ou"]] = (s_trf, vv)

            sp_head_loads(0)
            for _ in range(3):
                sp_kv_load()
            for k in range(min(LA, n_t)):
                emit_qk(k)
            out_units = [u for u, (kind, h, mp) in enumerate(units) if kind == "m" or mp == 1]
            pend_tr = []
            for k in range(n_t):
                cur["idx"] = k
                u, J, m, ci, c0, c1, lastc = items[k]
                if k + LA < n_t:
                    emit_qk(k + LA)
                emit_exp(k)
                flush(act_q, k)
                emit_pv(k)
                flush(dve_q, k)
                if J == 7 and m == 7 and lastc:
                    while pend_tr:
                        emit_transposes(pend_tr.pop(0))
                    if u in out_units:
                        pend_tr.append(u)
            while dve_q or act_q:
                flush(dve_q, n_t, True)
                flush(act_q, n_t, True)
            while pend_tr:
                emit_transposes(pend_tr.pop(0))
            ast["ti"] = ti0 + n_t
            ast["ug"] += len(units)
            return ast["val_pv"][ti0 + n_t - 1]

        nblk_global = {"n": 0}

        def next_wload():
            n = nblk_global["n"]
            l, b = divmod(n, NBLK)
            if l >= depth:
                return
            nblk_global["n"] += 1
            if b == B_WO and wo_gate.get(l) is not None:
                wait(POOL, wo_gate[l][0], wo_gate[l][1])
            wseq[(l, b)] = wload(l, b)

        wo_gate = {}
        issued = {"n": 0}

        def ensure_loaded(l, b):
            target = l * NBLK + b
            while nblk_global["n"] <= target:
                next_wload()

        def block_done(l, b, semval):
            wfree(l, b, semval)

        wait(POOL, s_dve, pre_done_dve)
        v_kr_free = None
        attn_last_pv = None
        def chk(name):
            if stop == name:
                raise _Stop()

        def layer_body(l):
            nonlocal v_kr_free, attn_last_pv
            chk('pre')
            ksA = kvsrcA[l % 2].ap()
            ksB = kvsrcB[l % 2].ap()
            if attn_last_pv is not None:
                pass
            ensure_loaded(l, 0)
            ensure_loaded(l, 1)
            if v_kr_free is not None:
                wait(POOL, v_kr_free[0], v_kr_free[1])
            v_wkr = DMA(POOL, wkr_t, wkr_d[l], s_wkr)

            def h_out(kc, hf, l=l):
                STT(DVE, hT[:, kc, hf * 512:(hf + 1) * 512], xT[:, kc, hf * 512:(hf + 1) * 512],
                    gx[:, l * KC + kc:l * KC + kc + 1], rstd[:, hf * 512:(hf + 1) * 512], ALU.mult, ALU.mult,
                    inc=(s_dve if (kc == KC - 1) else None))
            x_stats_and_hT(l, None, None, h_out)
            v_h = s_dve.n
            rstd_last_read["v"] = (s_dve, v_h)
            wait(PE, s_dve, v_h)

            kv_store_vals = []

            def fm_block_to(l, b, nheads, dst_fn, src=hT, kcn=KC, W3=None, cstep=128, c_base=0):
                vlast = None
                for hh in range(nheads):
                    si = None
                    for hf in range(2):
                        g, o, v = fm_group(W3, kcn, c_base + hh * cstep, 128, src, hf)
                        if hf == 0:
                            si = stb_acquire(DVE)
                        wait(DVE, s_mm, v)
                        vd = CP(DVE, stb[si][:, hf * 512:(hf + 1) * 512], o, inc=s_dve)
                        group_end(g, [(s_dve, vd)])
                        vlast = v
                    stb_store(si, dst_fn(hh), stb[si][:, :], (s_dve, vd))
                return vlast

            def tm_block_to(l, nj_cols, rhs_fn_maker, kcn, src, dst_fn, silu):
                vlast = None
                for j in range(NJ):
                    g, o, v = tm_group(rhs_fn_maker, kcn, src, j)
                    if silu:
                        si = stf_acquire(ACT)
                        wait(ACT, s_mm, v)
                        va = ACTV(stf[si][:, :], o, AF.Silu, inc=s_act)
                        group_end(g, [(s_act, va)])
                        stf_store(si, dst_fn(j), stf[si][:, :], (s_act, va))
                    else:
                        si = stb_acquire(DVE)
                        wait(DVE, s_mm, v)
                        vd = CP(DVE, stb[si][:, 0:512], o, inc=s_dve)
                        group_end(g, [(s_dve, vd)])
                        stb_store(si, dst_fn(j), stb[si][:, 0:512], (s_dve, vd))
                    vlast = v
                return vlast

            def w16(l, b):
                ensure_loaded(l, b)
                return wuse(l, b).rearrange("p (k c) -> p k c", k=16)

            def w4(l, b):
                ensure_loaded(l, b)
                return wuse(l, b).rearrange("p (k c) -> p k c", k=4)

            for bi in range(2):
                W3 = w16(l, B_KA + bi)
                v = fm_block_to(l, B_KA + bi, 4, lambda hh, bi=bi: ksA[:, KA_OFF + (bi * 4 + hh) * 1024:KA_OFF + (bi * 4 + hh + 1) * 1024], W3=W3)
                block_done(l, B_KA + bi, (s_mm, v))
                ensure_loaded(l, B_KA + bi + 2)
            chk('ka')
            for bi in range(2):
                W3 = w16(l, B_VA + bi)
                v = tm_block_to(l, 512, lambda kc, W3=W3: W3[:, kc, :], KC, hT,
                                lambda j, bi=bi: ksA[:, VA_OFF + j * 1024 + bi * 512:VA_OFF + j * 1024 + bi * 512 + 512], False)
                block_done(l, B_VA + bi, (s_mm, v))
                ensure_loaded(l, B_VA + bi + 2)
            chk('va')
            for i in range(4):
                wait(POOL, s_stb[i], stb_state["last"][i])
            emit(POOL, lambda e, l=l: e.collective_compute("AllGather", ALU.bypass, replica_groups=[list(range(NCORES))],
                                                           ins=[kvsrcA[l % 2].ap().opt()], outs=[kvdstA[l % 2].ap().opt()]),
                 s_ccA[l], 1)
            W3 = w16(l, B_CKV)

            def after_ckv(l=l):
                block_done(l, B_CKV, (s_mm, s_mm.n))
                ensure_loaded(l, B_CKV + 2)
            v_ckvn = c_stage(l, W3, gkv, after_ckv)
            wait(PE, s_dve, v_ckvn)
            chk('ckv')
            wait(PE, s_wkr, v_wkr)
            Wk = wkr_t.rearrange("p (k c) -> p k c", k=16)
            si = stb_acquire(DVE)
            vd = None
            for hf in range(2):
                gA, oA, vA = fm_group(Wk, KC, 0, 64, hT, hf)
                gB, oB, vB = fm_group(Wk, KC, 64, 64, hT, hf)
                vd = rope_evac(gA, oA, vA, gB, oB, vB, hf, stb[si][0:64, hf * 512:(hf + 1) * 512])
            v_kr_free = (s_mm, s_mm.n)
            stb_store(si, ksB[0:64, KR_OFF:KR_OFF + 1024], stb[si][0:64, :], (s_dve, vd))
            chk('kr')
            W4 = w4(l, B_WUKV)
            fm_block_to(l, B_WUKV, 8, lambda hh: ksB[:, KN_OFF + hh * 1024:KN_OFF + (hh + 1) * 1024],
                        src=cn, kcn=4, W3=W4, cstep=256, c_base=0)
            for gq_ in range(2):
                def rhs_fn(kc, gq_=gq_, W4=W4):
                    return W4[:, kc, :].rearrange("p (h e) -> p h e", e=256)[:, 4 * gq_:4 * gq_ + 4, 128:256]
                v = tm_block_to(l, 512, rhs_fn, 4, cn,
                                lambda j, gq_=gq_: ksB[:, VB_OFF + j * 1024 + gq_ * 512:VB_OFF + j * 1024 + gq_ * 512 + 512], False)
            block_done(l, B_WUKV, (s_mm, v))
            ensure_loaded(l, B_WUKV + 2)
            chk('wukv')
            kvB_store_vals = list(stb_state["last"])
            chk('ag')
            for bi in range(2):
                W3 = w16(l, B_QA + bi)
                v = fm_block_to(l, B_QA + bi, 4, lambda hh, bi=bi: qbuf[:, bi * 4 + hh, :], W3=W3)
                block_done(l, B_QA + bi, (s_mm, v))
                ensure_loaded(l, B_QA + bi + 2)
            chk('qa')
            for bi in range(2):
                W3 = w16(l, B_ZA + bi)
                v = tm_block_to(l, 512, lambda kc, W3=W3: W3[:, kc, :], KC, hT,
                                lambda j, bi=bi: zbuf[:, j, bi * 512:bi * 512 + 512], True)
                block_done(l, B_ZA + bi, (s_mm, v))
                ensure_loaded(l, B_ZA + bi + 2)
            for i in range(4):
                wait(POOL, s_stb[i], kvB_store_vals[i])
            wait(POOL, s_ccA[l], 1)
            emit(POOL, lambda e, l=l: e.collective_compute("AllGather", ALU.bypass, replica_groups=[list(range(NCORES))],
                                                           ins=[kvsrcB[l % 2].ap().opt()], outs=[kvdstB[l % 2].ap().opt()]),
                 s_ccB[l], 1)
            chk('za')
            W3 = w16(l, B_CQ)

            def after_cq(l=l):
                block_done(l, B_CQ, (s_mm, s_mm.n))
                ensure_loaded(l, B_CQ + 2)
            v_cn = c_stage(l, W3, gq, after_cq)
            wait(PE, s_dve, v_cn)
            W4 = w4(l, B_WUQ)
            fm_block_to(l, B_WUQ, 8, lambda hh: qbuf[:, 8 + hh, :], src=cn, kcn=4, W3=W4, cstep=192, c_base=0)
            for hh in range(8):
                si = stb_acquire(DVE)
                vd = None
                for hf in range(2):
                    gA, oA, vA = fm_group(W4, 4, hh * 192 + 128, 64, cn, hf)
                    gB, oB, vB = fm_group(W4, 4, 1536 + hh * 64, 64, cn, hf)
                    vd = rope_evac(gA, oA, vA, gB, oB, vB, hf, stb[si][0:64, hf * 512:(hf + 1) * 512])
                stb_store(si, qbuf[0:64, 16 + hh, :], stb[si][0:64, :], (s_dve, vd))
            block_done(l, B_WUQ, (s_mm, s_mm.n))
            ensure_loaded(l, B_WUQ + 2)
            for bi in range(2):
                W3 = w16(l, B_ZB + bi)
                v = tm_block_to(l, 512, lambda kc, W3=W3: W3[:, kc, :], KC, hT,
                                lambda j, bi=bi: zbuf[:, j, 1024 + bi * 512:1024 + bi * 512 + 512], True)
                block_done(l, B_ZB + bi, (s_mm, v))
            v_p1_last_mm = s_mm.n
            cn_last_read = (s_mm, s_mm.n)
            for i in range(4):
                wait(SP, s_stb[i], stb_state["last"][i])
            for i in range(2):
                wait(SP, s_stf[i], stf_state["last"][i])
            drain_groups()
            if debug and l == 0:
                DMA(SP, dbg_q, qbuf, s_out)
                DMA(SP, dbg_z, zbuf, s_out)
                wait(SP, s_out, s_out.n)

            chk('p1')
            attn_last_pv = attention_layer(l, (s_ccA[l], 1), (s_ccB[l], 1), [(s_mm, v_p1_last_mm), (s_dve, v_cn)])
            if debug and l == 0:
                wait(SP, s_trf, ast["trf"])
                DMA(SP, dbg_m, mixedT[:], s_out)
                wait(SP, s_out, s_out.n)

            chk('attn')
            wo_gate[l] = (s_pv, attn_last_pv)
            wait(PE, s_accfree, 8 * ast["ug"])
            wait(PE, s_trf, ast["trf"])
            vd = None
            for bi in range(4):
                W3 = w16(l, B_WO + bi)
                v = None
                for dmc in range(4):
                    for hf in range(2):
                        g, o, v = fm_group(W3, KC, dmc * 128, 128, mixedT, hf)
                        wait(DVE, s_mm, v)
                        xs = xT[:, bi * 4 + dmc, hf * 512:(hf + 1) * 512]
                        vd = TT(DVE, xs, o, xs, ALU.add, inc=s_dve)
                        group_end(g, [(s_dve, vd)])
                block_done(l, B_WO + bi, (s_mm, v))
                if bi < 2:
                    ensure_loaded(l, B_WO + bi + 2)
                elif l + 1 < depth:
                    ensure_loaded(l + 1, bi - 2)
            x_ready["v"] = (s_dve, vd)
            if debug and l == 0:
                wait(SP, s_dve, vd)
                DMA(SP, dbg_x, xT[:], s_out)
                wait(SP, s_out, s_out.n)

        try:
            for l in range(depth):
                layer_body(l)
        except _Stop:
            pass

        yT_v = yT_d.rearrange("(kc p) t -> p kc t", p=128)

        def y_out(kc, hf):
            si = stf_acquire(DVE)
            v = STT(DVE, stf[si][:, :], xT[:, kc, hf * 512:(hf + 1) * 512], gfin[:, kc:kc + 1],
                    rstd[:, hf * 512:(hf + 1) * 512], ALU.mult, ALU.mult, inc=s_dve)
            stf_store(si, yT_v[:, kc, hf * 512:(hf + 1) * 512], stf[si][:, :], (s_dve, v))
        x_stats_and_hT(depth, None, None, y_out)
        for i in range(2):
            wait(SP, s_stf[i], stf_state["last"][i])
        wait(SP, s_out, s_out.n)

        engs = [PE, ACT, DVE, POOL, SP]
        cnt = {}
        pc = [0] * len(engs)
        progressed = True
        while progressed:
            progressed = False
            for ei, E in enumerate(engs):
                while pc[ei] < len(E.meta):
                    kind, sem, val = E.meta[pc[ei]]
                    if kind == "wait":
                        if cnt.get(id(sem), 0) < val:
                            break
                    elif sem is not None:
                        cnt[id(sem)] = cnt.get(id(sem), 0) + val
                    pc[ei] += 1
                    progressed = True
        for ei, E in enumerate(engs):
            if pc[ei] < len(E.meta):
                kind, sem, val = E.meta[pc[ei]]
                nm = [k for k, v in sems.items() if v is sem]
                raise RuntimeError(f"DEADLOCK: engine {E.name} stuck at op {pc[ei]}/{len(E.meta)} waiting {nm} >= {val} (have {cnt.get(id(sem), 0)})")
        print("sync check ok:", {E.name: len(E.meta) for E in engs})

        with nc.Block() as block:
            @block.tensor
            def _(e):
                for f in PE.ops:
                    f(e)

            @block.scalar
            def _(e):
                for f in ACT.ops:
                    f(e)

            @block.vector
            def _(e):
                for f in DVE.ops:
                    f(e)

            @block.gpsimd
            def _(e):
                for f in POOL.ops:
                    f(e)

            @block.sync
            def _(e):
                for f in SP.ops:
                    f(e)
    return nc


def _t5_bucket_np(dist):
    n = np.maximum(dist, 0)
    nf = np.maximum(n, 1).astype(np.float32)
    large = 16 + (np.log(nf / np.float32(16)) / np.float32(math.log(128 / 16)) * np.float32(16)).astype(np.int32)
    large = np.minimum(large, 31)
    return np.where(n < 16, n, large)


def _pack16(w):
    return np.ascontiguousarray(w.reshape(16, 128, w.shape[1]).transpose(1, 0, 2)).reshape(128, -1)


def _pack4(w):
    return np.ascontiguousarray(w.reshape(4, 128, w.shape[1]).transpose(1, 0, 2)).reshape(128, -1)


_NC_CACHE = {}


def prepare_inputs(x, norm_g, w_in, diff_lambda, diff_subln_g, rel_bias_table, mla_q_norm_g, w_uq,
                   mla_kv_norm_g, w_ukv, w_out, final_norm_g, depth=DEPTH):
    f32 = np.float32
    x = np.asarray(x, f32); w_in = np.asarray(w_in, f32); w_uq = np.asarray(w_uq, f32)
    w_ukv = np.asarray(w_ukv, f32); w_out = np.asarray(w_out, f32)
    L = depth
    wpack = np.empty((depth, NBLK, 128, 8192), f32)
    wkr = np.empty((depth, 128, 2048), f32)
    swap64 = (np.arange(64) + 32) % 64
    for l in range(L):
        wi = w_in[l]
        wpack[l, B_KA + 0] = _pack16(wi[:, 1024:1536]); wpack[l, B_KA + 1] = _pack16(wi[:, 1536:2048])
        wpack[l, B_VA + 0] = _pack16(wi[:, 2048:2560]); wpack[l, B_VA + 1] = _pack16(wi[:, 2560:3072])
        wpack[l, B_CKV] = _pack16(wi[:, 4608:5120])
        wpack[l, B_WUKV] = _pack4(w_ukv[l])
        wpack[l, B_QA + 0] = _pack16(wi[:, 0:512]); wpack[l, B_QA + 1] = _pack16(wi[:, 512:1024])
        wpack[l, B_ZA + 0] = _pack16(wi[:, 3072:3584]); wpack[l, B_ZA + 1] = _pack16(wi[:, 3584:4096])
        wpack[l, B_CQ] = _pack16(wi[:, 4096:4608])
        rope_cols = (np.arange(8)[:, None] * 192 + 128 + swap64[None, :]).reshape(-1)
        wpack[l, B_WUQ] = _pack4(np.concatenate([w_uq[l], w_uq[l][:, rope_cols]], axis=1))
        wpack[l, B_ZB + 0] = _pack16(wi[:, 5184:5696]); wpack[l, B_ZB + 1] = _pack16(wi[:, 5696:6208])
        for bi in range(4):
            wpack[l, B_WO + bi] = _pack16(w_out[l][:, bi * 512:(bi + 1) * 512])
        kr = wi[:, 5120:5184]
        wkr[l] = _pack16(np.concatenate([kr, kr[:, swap64]], axis=1))

    def rep(v):
        v = np.asarray(v, f32).reshape(1, -1)
        return np.ascontiguousarray(np.repeat(v, 128, axis=0))

    def pmaj(g, nch):
        g = np.asarray(g, f32).reshape(-1, nch, 128)
        return np.ascontiguousarray(g.transpose(2, 0, 1)).reshape(128, -1)

    common = {
        "wpack": wpack, "wkr": wkr,
        "gx": pmaj(norm_g, KC), "gq": pmaj(mla_q_norm_g, 4), "gkv": pmaj(mla_kv_norm_g, 4),
        "gfin": pmaj(np.asarray(final_norm_g, f32)[None], KC),
        "subln": rep(diff_subln_g), "lam": rep(diff_lambda),
        "b31": rep(np.concatenate([np.asarray(rel_bias_table, f32)[31], np.zeros(1, f32)])),
        "ident": np.eye(128, dtype=f32),
    }
    table = np.asarray(rel_bias_table, f32)
    inv = (np.float32(1.0) / (np.float32(10000.0) ** (np.arange(0, 64, 2, dtype=f32) / np.float32(64)))).astype(f32)
    kk = np.arange(128)[:, None]
    qq = np.arange(128)[None, :]
    xs = x[0].reshape(NJ, NCORES, 128, D)
    in_maps = []
    for c in range(NCORES):
        xc = xs[:, c].reshape(T, D)
        pos = ((np.arange(NJ)[:, None] * NCORES + c) * 128 + np.arange(128)[None, :]).reshape(-1).astype(f32)
        ang = (pos[:, None] * inv[None, :]).astype(f32)
        ang = np.concatenate([ang, ang], axis=1)
        cosT = np.ascontiguousarray(np.cos(ang).astype(f32).T)
        sn = np.sin(ang).astype(f32)
        sn[:, :32] = -sn[:, :32]
        sinT = np.ascontiguousarray(sn.T)
        bm = np.empty((128, 9, 9, 128), f32)
        for slot in range(9):
            delta = (c - slot) if slot < 8 else (c + 1)
            dist = delta * 128 + qq - kk
            bidx = _t5_bucket_np(dist)
            masked = dist < 0
            for h in range(8):
                vals = table[bidx, h]
                bm[:, h, slot, :] = np.where(masked, f32(NEG), vals)
            bm[:, 8, slot, :] = np.where(masked, f32(NEG), f32(0.0))
        m = dict(common)
        m.update({"xT": np.ascontiguousarray(xc.T), "cosT": cosT, "sinT": sinT, "bm": bm.reshape(128, 9, 9 * 128)})
        in_maps.append(m)
    return in_maps


def assemble(results, key="yT"):
    out = np.empty((NJ, NCORES, 128, D), np.float32)
    for c in range(NCORES):
        yT = np.asarray(results[c][key], np.float32)
        out[:, c] = yT.T.reshape(NJ, 128, D)
    return out.reshape(1, SEQ, D)


def kernel(**inputs):
    in_maps = prepare_inputs(**inputs)
    if "nc" not in _NC_CACHE:
        _NC_CACHE["nc"] = build(DEPTH)
    res = run_bass_kernel_spmd(_NC_CACHE["nc"], in_maps, core_ids=list(range(NCORES)))
    return assemble(res.results)
```
